# Optimizing a Trainium2 kernel written in Bass

```python
import math
import jax, jax.numpy as jnp
from jax import lax
import numpy as np

D_MODEL = 1024
BATCH = 8
SEQ = 2048
DEPTH = 4
DEC_BATCH = 128
DEC_SEQ = 1
PAST_LEN = 16384
PAGE_SIZE = 128

N_M_HEADS = 4
M_HEAD_DIM = D_MODEL // N_M_HEADS
M_WIDTH = N_M_HEADS * M_HEAD_DIM
CHUNK = 64
POOL_WINDOWS = (2, 4, 8, 16)
N_POOL_GROUPS = len(POOL_WINDOWS)
POOL_GROUP_DIM = D_MODEL // N_POOL_GROUPS
POOL_WIDTH = N_POOL_GROUPS * POOL_GROUP_DIM
POOL_BUF = max(POOL_WINDOWS) - 1
N_MEM = 256
N_X_HEADS = 4
X_HEAD_DIM = D_MODEL // N_X_HEADS
X_WIDTH = N_X_HEADS * X_HEAD_DIM
D_FF = ((8 * D_MODEL // 3 + 127) // 128) * 128
CONV_WIDTH = 3
N_BRANCH = 3
ALPHA = (2.0 * DEPTH) ** 0.25
BETA = (8.0 * DEPTH) ** -0.25
LN_EPS = 1e-5
NEG = -1e30
IN_SPLITS = (M_WIDTH, M_WIDTH, M_WIDTH, M_WIDTH, N_M_HEADS, N_M_HEADS, POOL_WIDTH, X_WIDTH, N_BRANCH * D_MODEL)
IN_WIDTH = sum(IN_SPLITS)
IN_OFFSETS = tuple(int(v) for v in np.cumsum(IN_SPLITS)[:-1])
F_GATE_OFF = 4 * M_WIDTH + N_M_HEADS

kernel_name = "hybrid_mlstm_pool_memxattn_step"


def layer_norm(x, g, b):
    xf = x.astype(jnp.float32)
    mu = jnp.mean(xf, axis=-1, keepdims=True)
    var = jnp.mean(jnp.square(xf - mu), axis=-1, keepdims=True)
    return ((xf - mu) * lax.rsqrt(var + LN_EPS) * g.astype(jnp.float32) + b.astype(jnp.float32)).astype(x.dtype)


def mlstm_chunkwise(q, k, v, ig, lf, C0, n0, m0):
    B, L, H, Dk = q.shape
    f32 = jnp.float32
    Lc = min(CHUNK, L)
    nc = -(-L // Lc)
    pad = nc * Lc - L
    q = q.astype(f32)
    k = k.astype(f32) * (Dk ** -0.5)
    v = v.astype(f32)
    ig = ig.astype(f32)
    lf = lf.astype(f32)
    if pad:
        p4 = ((0, 0), (0, pad), (0, 0), (0, 0))
        p3 = ((0, 0), (0, pad), (0, 0))
        q, k, v = jnp.pad(q, p4), jnp.pad(k, p4), jnp.pad(v, p4)
        ig = jnp.pad(ig, p3, constant_values=NEG)
        lf = jnp.pad(lf, p3)

    def to_chunks(t):
        return jnp.moveaxis(t.reshape((B, nc, Lc) + t.shape[2:]), 1, 0)

    causal = jnp.tril(jnp.ones((Lc, Lc), dtype=bool))[None, :, :, None]

    def step(carry, inp):
        C, n, m = carry
        qc, kc, vc, ic, fc = inp
        b = jnp.cumsum(fc, axis=1)
        a = ic - b
        mt = b + jnp.maximum(m[:, None, :], lax.cummax(a, axis=1))
        inter = jnp.exp(m[:, None, :] + b - mt)
        logd = a[:, None, :, :] + b[:, :, None, :] - mt[:, :, None, :]
        dmat = jnp.exp(jnp.where(causal, logd, NEG))
        s = jnp.einsum('bthd,bshd->btsh', qc, kc) * dmat
        num = jnp.einsum('btsh,bshd->bthd', s, vc) + inter[..., None] * jnp.einsum('bthd,bhde->bthe', qc, C)
        den = jnp.sum(s, axis=2) + inter * jnp.einsum('bthd,bhd->bth', qc, n)
        h = num / jnp.maximum(jnp.abs(den), jnp.exp(-mt))[..., None]
        mL = mt[:, -1]
        bL = b[:, -1]
        ws = jnp.exp(a + bL[:, None] - mL[:, None])
        decay = jnp.exp(m + bL - mL)
        kw = kc * ws[..., None]
        C_new = decay[..., None, None] * C + jnp.einsum('bshd,bshe->bhde', kw, vc)
        n_new = decay[..., None] * n + jnp.sum(kw, axis=1)
        return (C_new, n_new, mL), h

    (C, n, m), h = lax.scan(step, (C0.astype(f32), n0.astype(f32), m0.astype(f32)),
                            tuple(to_chunks(t) for t in (q, k, v, ig, lf)))
    h = jnp.moveaxis(h, 0, 1).reshape(B, nc * Lc, H, Dk)[:, :L]
    return h, C, n, m


def pool_mix(u, buf, pos0, w_pool, pool_scale):
    B, L, _ = u.shape
    P = POOL_BUF
    ext = jnp.concatenate([buf.astype(u.dtype), u], axis=1)
    extf = ext.astype(jnp.float32)
    cs = jnp.concatenate([jnp.zeros((B, 1, POOL_WIDTH), jnp.float32), jnp.cumsum(extf, axis=1)], axis=1)
    pos = pos0 + jnp.arange(L, dtype=jnp.int32)
    outs = []
    for g, w in enumerate(POOL_WINDOWS):
        lo, hi = g * POOL_GROUP_DIM, (g + 1) * POOL_GROUP_DIM
        wsum = cs[:, P + 1:P + 1 + L, lo:hi] - cs[:, P + 1 - w:P + 1 - w + L, lo:hi]
        cnt = jnp.minimum(w, pos + 1).astype(jnp.float32)[None, :, None]
        outs.append(wsum / cnt - extf[:, P:, lo:hi])
    d = jnp.stack(outs, axis=2).astype(u.dtype)
    y = jnp.einsum('blgc,gce->blge', d, w_pool).reshape(B, L, POOL_WIDTH) * pool_scale
    return y, ext[:, -P:]


def mem_kv(mem, w_kv):
    B = mem.shape[0]
    kv = (mem @ w_kv).reshape(B, N_MEM, 2, N_X_HEADS, X_HEAD_DIM)
    return kv[:, :, 0], kv[:, :, 1]


def mem_xattn(xq, mk, mv, w_x_out):
    B, L, _ = xq.shape
    q = xq.reshape(B, L, N_X_HEADS, X_HEAD_DIM).astype(jnp.float32)
    s = jnp.einsum('blhd,bmhd->bhlm', q, mk.astype(jnp.float32)) * (X_HEAD_DIM ** -0.5)
    p = jax.nn.softmax(s, axis=-1)
    o = jnp.einsum('bhlm,bmhd->blhd', p, mv.astype(jnp.float32)).astype(xq.dtype)
    return o.reshape(B, L, X_WIDTH) @ w_x_out


def conv_ffn(x, buf, w_up, conv_w, conv_b, w_down):
    L = x.shape[1]
    hup = x @ w_up
    ext = jnp.concatenate([buf.astype(hup.dtype), hup], axis=1)
    c = ext[:, 0:L] * conv_w[0] + ext[:, 1:L + 1] * conv_w[1] + ext[:, 2:L + 2] * conv_w[2] + conv_b
    a, b = jnp.split(c, 2, axis=-1)
    return (jax.nn.gelu(a) * b) @ w_down, ext[:, -(CONV_WIDTH - 1):]


def trunk_layer(x, mk, mv, C0, n0, m0, pbuf, cbuf, pos0,
                w_in, b_in, mh_g, w_m_out, w_pool, pool_scale, w_x_out, w_o,
                ln1_g, ln1_b, w_up, conv_w, conv_b, w_down, ln2_g, ln2_b):
    B, L, _ = x.shape
    proj = x @ w_in + b_in
    q, k, v, o, ig, fg, u, xq, gates = jnp.split(proj, IN_OFFSETS, axis=-1)
    hs = lambda t: t.reshape(B, L, N_M_HEADS, M_HEAD_DIM)
    lf = jax.nn.log_sigmoid(fg.astype(jnp.float32))
    ht, C, n, m = mlstm_chunkwise(hs(q), hs(k), hs(v), ig, lf, C0, n0, m0)
    hc = jax.nn.sigmoid(hs(o).astype(jnp.float32)) * ht
    mu = jnp.mean(hc, axis=-1, keepdims=True)
    var = jnp.mean(jnp.square(hc - mu), axis=-1, keepdims=True)
    hn = (hc - mu) * lax.rsqrt(var + LN_EPS) * mh_g.reshape(N_M_HEADS, M_HEAD_DIM).astype(jnp.float32)
    y_m = hn.reshape(B, L, M_WIDTH).astype(x.dtype) @ w_m_out
    y_p, pbuf_new = pool_mix(u, pbuf, pos0, w_pool, pool_scale)
    y_x = mem_xattn(xq, mk, mv, w_x_out)
    g = jax.nn.sigmoid(gates.reshape(B, L, N_BRANCH, D_MODEL))
    mix = (g[:, :, 0] * y_m + g[:, :, 1] * y_p + g[:, :, 2] * y_x) @ w_o
    x = layer_norm(ALPHA * x + mix, ln1_g, ln1_b)
    f, cbuf_new = conv_ffn(x, cbuf, w_up, conv_w, conv_b, w_down)
    x = layer_norm(ALPHA * x + f, ln2_g, ln2_b)
    return (x, C.astype(C0.dtype), n.astype(n0.dtype), m.astype(m0.dtype),
            pbuf_new.astype(pbuf.dtype), cbuf_new.astype(cbuf.dtype))


def setup_inputs(seed: int = 0) -> dict:
    key = jax.random.key(seed)
    ks = iter(jax.random.split(key, 40))

    def nrm(shape, scale):
        return scale * jax.random.normal(next(ks), shape, jnp.float32)

    d = {}
    d['x_prompt'] = nrm((BATCH, SEQ, D_MODEL), 1.0)
    d['mem_prompt'] = nrm((BATCH, N_MEM, D_MODEL), 1.0)
    d['x_sample'] = nrm((DEC_BATCH, DEC_SEQ, D_MODEL), 1.0)
    d['cache_mem_k'] = nrm((DEPTH, DEC_BATCH, N_MEM, N_X_HEADS, X_HEAD_DIM), 1.0)
    d['cache_mem_v'] = nrm((DEPTH, DEC_BATCH, N_MEM, N_X_HEADS, X_HEAD_DIM), 1.0)
    d['state_C'] = nrm((DEPTH, DEC_BATCH, N_M_HEADS, M_HEAD_DIM, M_HEAD_DIM), 1.0)
    d['state_n'] = nrm((DEPTH, DEC_BATCH, N_M_HEADS, M_HEAD_DIM), 1.0)
    d['state_m'] = nrm((DEPTH, DEC_BATCH, N_M_HEADS), 1.0)
    d['state_pool'] = nrm((DEPTH, DEC_BATCH, POOL_BUF, POOL_WIDTH), 1.0)
    d['state_conv'] = nrm((DEPTH, DEC_BATCH, CONV_WIDTH - 1, 2 * D_FF), 1.0)
    d['ln_in_g'] = 1.0 + nrm((D_MODEL,), 0.02)
    d['ln_in_b'] = nrm((D_MODEL,), 0.02)
    d['w_in'] = nrm((DEPTH, D_MODEL, IN_WIDTH), D_MODEL ** -0.5)
    b_in = nrm((DEPTH, IN_WIDTH), 0.02)
    d['b_in'] = b_in.at[:, F_GATE_OFF:F_GATE_OFF + N_M_HEADS].add(jnp.linspace(3.0, 6.0, N_M_HEADS))
    d['mh_g'] = 1.0 + nrm((DEPTH, M_WIDTH), 0.02)
    d['w_m_out'] = nrm((DEPTH, M_WIDTH, D_MODEL), M_WIDTH ** -0.5)
    d['w_pool'] = nrm((DEPTH, N_POOL_GROUPS, POOL_GROUP_DIM, POOL_GROUP_DIM), POOL_GROUP_DIM ** -0.5)
    d['pool_scale'] = 1.0 + nrm((DEPTH, POOL_WIDTH), 0.02)
    d['w_mem_kv'] = nrm((DEPTH, D_MODEL, 2 * X_WIDTH), D_MODEL ** -0.5)
    d['w_x_out'] = nrm((DEPTH, X_WIDTH, D_MODEL), X_WIDTH ** -0.5)
    d['w_o'] = nrm((DEPTH, D_MODEL, D_MODEL), BETA * D_MODEL ** -0.5)
    d['ln1_g'] = 1.0 + nrm((DEPTH, D_MODEL), 0.02)
    d['ln1_b'] = nrm((DEPTH, D_MODEL), 0.02)
    d['w_up'] = nrm((DEPTH, D_MODEL, 2 * D_FF), D_MODEL ** -0.5)
    d['conv_w'] = nrm((DEPTH, CONV_WIDTH, 2 * D_FF), CONV_WIDTH ** -0.5)
    d['conv_b'] = nrm((DEPTH, 2 * D_FF), 0.02)
    d['w_down'] = nrm((DEPTH, D_FF, D_MODEL), BETA * D_FF ** -0.5)
    d['ln2_g'] = 1.0 + nrm((DEPTH, D_MODEL), 0.02)
    d['ln2_b'] = nrm((DEPTH, D_MODEL), 0.02)
    return d


def reference(x_prompt, mem_prompt, x_sample, cache_mem_k, cache_mem_v, state_C, state_n, state_m,
              state_pool, state_conv, ln_in_g, ln_in_b, w_in, b_in, mh_g, w_m_out, w_pool, pool_scale,
              w_mem_kv, w_x_out, w_o, ln1_g, ln1_b, w_up, conv_w, conv_b, w_down, ln2_g, ln2_b):
    dt = x_prompt.dtype
    xp = layer_norm(x_prompt, ln_in_g, ln_in_b)
    xs = layer_norm(x_sample, ln_in_g, ln_in_b)
    pC, pn, pm, ppool, pconv, pmk, pmv = [], [], [], [], [], [], []
    sC, sn, sm, spool, sconv = [], [], [], [], []
    for l in range(DEPTH):
        w = (w_in[l], b_in[l], mh_g[l], w_m_out[l], w_pool[l], pool_scale[l], w_x_out[l], w_o[l],
             ln1_g[l], ln1_b[l], w_up[l], conv_w[l], conv_b[l], w_down[l], ln2_g[l], ln2_b[l])
        mk, mv = mem_kv(mem_prompt, w_mem_kv[l])
        xp, C1, n1, m1, pb1, cb1 = trunk_layer(
            xp, mk, mv,
            jnp.zeros((BATCH, N_M_HEADS, M_HEAD_DIM, M_HEAD_DIM), dt),
            jnp.zeros((BATCH, N_M_HEADS, M_HEAD_DIM), dt),
            jnp.zeros((BATCH, N_M_HEADS), dt),
            jnp.zeros((BATCH, POOL_BUF, POOL_WIDTH), dt),
            jnp.zeros((BATCH, CONV_WIDTH - 1, 2 * D_FF), dt),
            0, *w)
        pC.append(C1); pn.append(n1); pm.append(m1); ppool.append(pb1); pconv.append(cb1)
        pmk.append(mk); pmv.append(mv)
        xs, C2, n2, m2, pb2, cb2 = trunk_layer(
            xs, cache_mem_k[l], cache_mem_v[l], state_C[l], state_n[l], state_m[l],
            state_pool[l], state_conv[l], PAST_LEN, *w)
        sC.append(C2); sn.append(n2); sm.append(m2); spool.append(pb2); sconv.append(cb2)
    return (xp, xs,
            jnp.stack(pC), jnp.stack(pn), jnp.stack(pm), jnp.stack(ppool), jnp.stack(pconv),
            jnp.stack(pmk), jnp.stack(pmv),
            jnp.stack(sC), jnp.stack(sn), jnp.stack(sm), jnp.stack(spool), jnp.stack(sconv))
```

```python
import os
import contextlib
import numpy as np
import concourse.bass as bass
import concourse.mybir as mybir
from concourse.bass_utils import run_bass_kernel_spmd

F32 = mybir.dt.float32
BF16 = mybir.dt.bfloat16
AF = mybir.ActivationFunctionType
ALU = mybir.AluOpType
AX = mybir.AxisListType

D = 1024; KC = 8; H = 4; DH = 256; DEPTH = 4; SEQ = 2048; NS = 16; FF = 2816; FC = 22; NIN = 9224; NMEM = 256
NT = SEQ + NS
ALPHA = (2.0 * DEPTH) ** 0.25
EPS = 1e-5
OQ, OK_, OV, OO, OIG, OFG, OU, OXQ, OG = 0, 1024, 2048, 3072, 4096, 4100, 4104, 5128, 6152
LN16 = float(np.log(16.0))
GK = 0.7978845608028654
NEG = -30000.0
BQ, BK, BU, BXQ, BG, MHG, PSC, L1G, L1B, L2G, L2B, CW, CB, BIG, BFG, VL = 0, 8, 16, 24, 32, 56, 64, 72, 80, 88, 96, 104, 236, 280, 281, 282
LIG = DEPTH * VL
LIB = LIG + 8
NV = LIB + 8
C_ID, C_NEGM, C_SELH, C_RMASK, C_DIAG, C_RC, C_SELP, C_ONES, NCST = 0, 128, 256, 768, 1280, 1536, 1552, 1680, 1808
RD = 5
NOSELF = int(os.environ.get("KNOSELF", 0))


def _isap(v):
    return hasattr(v, "offset") and hasattr(v, "ap") and hasattr(v, "space")


class KB:
    NAMES = {}

    def __init__(s, depth):
        s.depth = depth
        s.nc = bass.Bass("TRN2", target_bir_lowering=False)
        nc = s.nc
        s.es = contextlib.ExitStack()
        s.E = {}
        for n, h in (("pe", nc.tensor), ("act", nc.scalar), ("dve", nc.vector), ("pool", nc.gpsimd), ("sp", nc.sync)):
            s.E[n] = dict(h=h, sem=s.es.enter_context(nc.semaphore("sem_" + n)), cnt=0, waited={})
        s.DQ = {}
        for q, k in (("sp", 12), ("pool", 6)):
            s.DQ[q] = dict(slots=[dict(sem=s.es.enter_context(nc.semaphore("dq_%s%d" % (q, i))), n=0) for i in range(k)], i=0)
        s.recs = {}
        s.scopes = [s.es]
        s.uid = 0

    def T(s, shape, dt, name=None):
        s.uid += 1
        full = "%s_%d" % (name or "t", s.uid)
        KB.NAMES.setdefault(name or "t", []).append(full)
        return s.scopes[-1].enter_context(s.nc.sbuf_tensor(full, list(shape), dt))

    def push(s):
        st = contextlib.ExitStack()
        s.scopes.append(st)

    def pop(s):
        s.barrier()
        s.scopes.pop().close()

    def region(s, ap):
        sp = str(ap.space)
        if "SB" not in sp and "PSUM" not in sp:
            return None
        if "PSUM" in sp:
            return (ap.name, 0, 2048)
        a = ap.ap
        es = 2 if ap.dtype == BF16 else 4
        pstep = a[0][0]
        off = ap.offset
        lo = off % pstep if pstep > 0 else off
        ext = 1
        for st, c in a[1:]:
            ext += (c - 1) * abs(st)
        return (ap.name, lo * es, (lo + ext) * es)

    def _wait(s, en, sem, val):
        e = s.E[en]
        k = id(sem)
        if e["waited"].get(k, 0) >= val:
            return
        e["h"].wait_ge(sem, val)
        e["waited"][k] = val

    def _depwait(s, en, reads, writes):
        need = {}
        for ap, isw in [(a, False) for a in reads] + [(a, True) for a in writes]:
            r = s.region(ap)
            if r is None:
                continue
            name, lo, hi = r
            for rec in s.recs.get(name, ()):
                if rec[0] < hi and lo < rec[1] and (isw or rec[2]):
                    if rec[5] == en and (en == "pe" or NOSELF):
                        continue
                    k = id(rec[3])
                    if k not in need or need[k][1] < rec[4]:
                        need[k] = (rec[3], rec[4])
        for sem, val in need.values():
            s._wait(en, sem, val)

    def _update(s, reads, writes, sem, val, en):
        for ap in writes:
            r = s.region(ap)
            if r is None:
                continue
            name, lo, hi = r
            L = s.recs.setdefault(name, [])
            L[:] = [x for x in L if not (lo <= x[0] and x[1] <= hi)]
            L.append([lo, hi, True, sem, val, en])
        for ap in reads:
            r = s.region(ap)
            if r is None:
                continue
            name, lo, hi = r
            L = s.recs.setdefault(name, [])
            for x in L:
                if (not x[2]) and x[0] == lo and x[1] == hi and x[5] == en:
                    x[3] = sem
                    x[4] = val
                    break
            else:
                L.append([lo, hi, False, sem, val, en])

    def op(s, en, fn, reads, writes):
        s._depwait(en, reads, writes)
        inst = fn()
        e = s.E[en]
        e["cnt"] += 1
        inst.then_inc(e["sem"], 1)
        s._update(reads, writes, e["sem"], e["cnt"], en)

    def barrier(s, final=False):
        for en, e in s.E.items():
            for on, o in s.E.items():
                if on != en and o["cnt"] > 0:
                    s._wait(en, o["sem"], o["cnt"])
            for qn, dq in s.DQ.items():
                if qn == "pool" and not final:
                    continue
                for sl in dq["slots"]:
                    if sl["n"] > 0:
                        s._wait(en, sl["sem"], 16 * sl["n"])
        s.recs = {} if final else {n_: r_ for n_, r_ in s.recs.items() if n_.startswith("wr")}

    def dma(s, q, out, in_, slow=False):
        e = s.E[q]
        dq = s.DQ[q]
        sl = dq["slots"][dq["i"] % len(dq["slots"])]
        dq["i"] += 1
        if sl["n"] > 0:
            s._wait(q, sl["sem"], 16 * sl["n"])
        s._depwait(q, [in_], [out])
        if slow:
            inst = e["h"].dma_start(out=out, in_=in_, allow_slow_non_contiguous=True)
        else:
            inst = e["h"].dma_start(out=out, in_=in_)
        sl["n"] += 1
        inst.then_inc(sl["sem"], 16)
        s._update([in_], [out], sl["sem"], 16 * sl["n"], "dma")

    def mm(s, out, lhsT, rhs, start=True, stop=True):
        s.op("pe", lambda: s.nc.tensor.matmul(out, lhsT, rhs, start=start, stop=stop), [lhsT, rhs], [out])

    def tr(s, out, in_, ident):
        s.op("pe", lambda: s.nc.tensor.transpose(out, in_, ident), [in_, ident], [out])

    def g(s, en, name, **kw):
        reads = [v for k, v in kw.items() if _isap(v) and k not in ("out", "accum_out")]
        writes = [v for k, v in kw.items() if _isap(v) and k in ("out", "accum_out")]
        h = s.E[en]["h"]
        s.op(en, lambda: getattr(h, name)(**kw), reads, writes)

    def act(s, out, in_, func, bias=None, scale=None, accum_out=None):
        kw = dict(out=out, in_=in_, func=func)
        if bias is not None:
            kw["bias"] = bias
        if scale is not None:
            kw["scale"] = scale
        if accum_out is not None:
            kw["accum_out"] = accum_out
        s.g("act", "activation", **kw)

    def tt(s, out, in0, in1, op, en="dve"):
        s.g(en, "tensor_tensor", out=out, in0=in0, in1=in1, op=op)

    def ts(s, out, in0, s1, s2=None, op0=ALU.mult, op1=None, en="dve"):
        kw = dict(out=out, in0=in0, scalar1=s1, scalar2=s2, op0=op0)
        if op1 is not None:
            kw["op1"] = op1
        s.g(en, "tensor_scalar", **kw)

    def stt(s, out, in0, scalar, in1, op0, op1):
        s.g("dve", "scalar_tensor_tensor", out=out, in0=in0, scalar=scalar, in1=in1, op0=op0, op1=op1)

    def cp(s, out, in_, en="dve"):
        if en == "act":
            s.act(out, in_, AF.Copy)
        else:
            s.g(en, "tensor_copy", out=out, in_=in_)

    def memset(s, ap, val, en="dve"):
        h = s.E[en]["h"]
        s.op(en, lambda: h.memset(ap, val), [], [ap])

    def red(s, out, in_, op):
        s.g("dve", "tensor_reduce", out=out, in_=in_, axis=AX.X, op=op)

    def recip(s, out, in_):
        s.g("dve", "reciprocal", out=out, in_=in_)


class _Stop(Exception):
    pass


def build(depth=DEPTH):
    k = KB(depth)
    nc = k.nc
    kstop = int(os.environ.get("KSTOP", -1))
    ckc = [0]

    def ck(tag):
        if kstop >= 0 and ckc[0] == kstop:
            print("KSTOP at", tag, flush=True)
            raise _Stop()
        ckc[0] += 1
    try:
        _build_body(k, nc, depth, ck)
    except _Stop:
        while len(k.scopes) > 1:
            k.pop()
        k.barrier(final=True)
    return nc


def _build_body(k, nc, depth, ck):

    mini = int(os.environ.get("KMINI", 0))

    def din(name, shape):
        if mini and name not in ("xs", "xp", "vecs", "cst", "mem"):
            shape = [2] + [2] * (len(shape) - 1)
        return nc.dram_tensor(name, list(shape), F32, kind="ExternalInput").ap()

    def dout(name, shape):
        return nc.dram_tensor(name, list(shape), F32, kind="ExternalOutput").ap()

    xp = din("xp", [SEQ, D]); mem = din("mem", [NMEM, D]); xs = din("xs", [NS, D])
    cmk = din("cmk", [DEPTH, NS, NMEM, D]); cmv = din("cmv", [DEPTH, NS, NMEM, D])
    sC = din("sC", [DEPTH, NS, H, DH, DH]); sn = din("sn", [DEPTH, NS, H * DH]); sm = din("sm", [DEPTH, NS, H])
    spool = din("spool", [DEPTH, NS, 15, D]); sconv = din("sconv", [DEPTH, NS, 2, 2 * FF])
    w_in = din("w_in", [DEPTH, D, NIN]); b_in = din("b_in", [DEPTH, NIN])
    w_m_out = din("w_m_out", [DEPTH, D, D]); w_pool = din("w_pool", [DEPTH, 4 * 256, 256])
    w_kv = din("w_kv", [DEPTH, D, 2 * D]); w_x_out = din("w_x_out", [DEPTH, D, D]); w_o = din("w_o", [DEPTH, D, D])
    w_up = din("w_up", [DEPTH, D, 2 * FF]); w_down = din("w_down", [DEPTH, FF, D])
    vecs_d = din("vecs", [128, NV]); cst_d = din("cst", [128, NCST])
    yp = dout("yp", [SEQ, D]); ys = dout("ys", [NS, D])
    pC = dout("pC", [DEPTH, H, DH, DH]); pn = dout("pn", [DEPTH, H * DH]); pm = dout("pm", [DEPTH, H])
    ppool = dout("ppool", [DEPTH, 15, D]); pconv = dout("pconv", [DEPTH, 2, 2 * FF])
    pmk = dout("pmk", [DEPTH, NMEM, D]); pmv = dout("pmv", [DEPTH, NMEM, D])
    sCo = dout("sCo", [DEPTH, NS, H, DH, DH]); sno = dout("sno", [DEPTH, NS, H * DH]); smo = dout("smo", [DEPTH, NS, H])
    spoolo = dout("spoolo", [DEPTH, NS, 15, D]); sconvo = dout("sconvo", [DEPTH, NS, 2, 2 * FF])

    es = k.es
    PS = [es.enter_context(nc.psum_tensor("ps%d" % i, [128, 512], F32)) for i in range(8)]
    PSB = [p[:, :].bitcast(BF16) for p in PS]

    xhi = k.T([128, KC, NT], BF16, "xhi"); xlo = k.T([128, KC, NT], BF16, "xlo")
    wring = [k.T([128, KC, 512], BF16, "wr%d" % i) for i in range(RD)]
    vecs = k.T([128, NV], F32, "vecs"); cst = k.T([128, NCST], F32, "cst")
    identb = k.T([128, 128], BF16, "identb"); onesb = k.T([128, 128], BF16, "onesb")
    Cst = k.T([128, H, 2, 257], F32, "Cst"); Cb = k.T([128, H, 2, 257], BF16, "Cb")
    mkT = k.T([128, KC, NMEM], BF16, "mkT"); mvb = k.T([128, 2, D], BF16, "mvb")
    uh = k.T([128, KC, 16], F32, "uh"); chh = k.T([128, 2 * FC, 2], F32, "chh")
    mprev = k.T([4, 1], F32, "mprev"); hbg = k.T([128, 24], F32, "hbg"); nbfg = k.T([4, 2], F32, "nbfg")

    k.dma("sp", vecs[:, :], vecs_d[:, :]); k.dma("sp", cst[:, :], cst_d[:, :])
    identf = cst[:, C_ID:C_ID + 128]; negm = cst[:, C_NEGM:C_NEGM + 128]
    k.cp(identb[:, :], identf); k.cp(onesb[:, :], cst[:, C_ONES:C_ONES + 128])

    def selh(h):
        return cst[0:4, C_SELH + h * 128:C_SELH + (h + 1) * 128]

    def wv(ap2d):
        return ap2d.rearrange("(kc p) n -> p kc n", p=128)

    sched = []
    for l in range(0 if mini else depth):
        for j in range(4):
            sched.append(("kv%d" % j, [(0, w_kv[l, :, j * 512:(j + 1) * 512])]))
        for p in range(5):
            sched.append(("sg", [(0, w_in[l, :, OIG:OIG + 8])]))
            for h in range(H):
                sched.append(("A%d" % h, [(0, w_in[l, :, OQ + h * 256:OQ + (h + 1) * 256]), (256, w_in[l, :, OK_ + h * 256:OK_ + (h + 1) * 256])]))
                sched.append(("B%d" % h, [(0, w_in[l, :, OV + h * 256:OV + (h + 1) * 256]), (256, w_in[l, :, OO + h * 256:OO + (h + 1) * 256])]))
            for j in range(2):
                sched.append(("wm%d" % j, [(0, w_m_out[l, :, j * 512:(j + 1) * 512])]))
                sched.append(("g0_%d" % j, [(0, w_in[l, :, OG + j * 512:OG + (j + 1) * 512])]))
            for j in range(2):
                sched.append(("u%d" % j, [(0, w_in[l, :, OU + j * 512:OU + (j + 1) * 512])]))
            sched.append(("wp", [(0, w_pool[l, :, :])]))
            for j in range(2):
                sched.append(("g1_%d" % j, [(0, w_in[l, :, OG + 1024 + j * 512:OG + 1024 + (j + 1) * 512])]))
            for j in range(2):
                sched.append(("xq%d" % j, [(0, w_in[l, :, OXQ + j * 512:OXQ + (j + 1) * 512])]))
            for j in range(2):
                sched.append(("wx%d" % j, [(0, w_x_out[l, :, j * 512:(j + 1) * 512])]))
                sched.append(("g2_%d" % j, [(0, w_in[l, :, OG + 2048 + j * 512:OG + 2048 + (j + 1) * 512])]))
            for j in range(2):
                sched.append(("wo%d" % j, [(0, w_o[l, :, j * 512:(j + 1) * 512])]))
            for j in range(11):
                sched.append(("up%d" % j, [(0, w_up[l, :, j * 256:(j + 1) * 256]), (256, w_up[l, :, FF + j * 256:FF + (j + 1) * 256])]))
            for cr in range(2):
                for kr in range(3):
                    sched.append(("dn%d%d" % (cr, kr), [(0, w_down[l, kr * 1024:min(FF, (kr + 1) * 1024), cr * 512:(cr + 1) * 512])]))
    st = dict(wi=0, wl=0)

    def issue_slab(i):
        slot = wring[i % RD]
        for c0, ap2 in sched[i][1]:
            nk = ap2.shape[0] // 128
            ncol = ap2.shape[1]
            k.dma("pool", slot[:, 0:nk, c0:c0 + ncol], wv(ap2))

    def wnext(tag):
        i = st["wi"]
        st["wi"] += 1
        assert sched[i][0] == tag, (sched[i][0], tag)
        while st["wl"] < min(len(sched), i + RD - 1):
            issue_slab(st["wl"])
            st["wl"] += 1
        return wring[i % RD]

    def V(l, c0, n=1):
        return vecs[:, l * VL + c0:l * VL + c0 + n]

    def proj_fm(W, wc0, xcols, n, ps, nk=KC):
        for kc in range(nk):
            k.mm(ps[:, 0:n], W[:, kc, wc0:wc0 + 128], xhi[:, kc, xcols], start=(kc == 0), stop=(kc == nk - 1))

    def fm_to_tm_out(srcs, M, dst2d, stg, q="sp", row0=0):
        nch = len(srcs)
        for b0 in range(0, nch, 4):
            nb = min(4, nch - b0)
            for i in range(nb):
                k.tr(PS[3][0:M, i * 128:(i + 1) * 128], srcs[b0 + i], identf)
            k.cp(stg[0:M, 0:nb * 128], PS[3][0:M, 0:nb * 128], en="act")
            k.dma(q, dst2d[:, b0 * 128:(b0 + nb) * 128], stg[row0:M, 0:nb * 128])

    def layer_norm(r, rb, n, gcol, bcol, xcols, tmp1, tmp2):
        k.cp(rb[:, :, 0:n], r[:, :, 0:n], en="act")
        for kc in range(KC):
            k.mm(PS[0][:, 0:n], onesb[:, :], rb[:, kc, 0:n], start=(kc == 0), stop=(kc == KC - 1))
        k.ts(tmp1[:, 0:n], PS[0][:, 0:n], -1.0 / D)
        k.tt(r[:, :, 0:n], r[:, :, 0:n], tmp1[:, 0:n].unsqueeze(1).broadcast_to([128, KC, n]), ALU.add)
        k.act(rb[:, :, 0:n], r[:, :, 0:n], AF.Square)
        for kc in range(KC):
            k.mm(PS[1][:, 0:n], onesb[:, :], rb[:, kc, 0:n], start=(kc == 0), stop=(kc == KC - 1))
        k.ts(tmp2[:, 0:n], PS[1][:, 0:n], 1.0 / D, EPS, ALU.mult, ALU.add)
        k.act(tmp2[:, 0:n], tmp2[:, 0:n], AF.Ln)
        k.act(tmp2[:, 0:n], tmp2[:, 0:n], AF.Exp, scale=-0.5)
        k.tt(r[:, :, 0:n], r[:, :, 0:n], tmp2[:, 0:n].unsqueeze(1).broadcast_to([128, KC, n]), ALU.mult)
        for kc in range(KC):
            k.act(r[:, kc, 0:n], r[:, kc, 0:n], AF.Identity, bias=vecs[:, bcol + kc:bcol + kc + 1], scale=vecs[:, gcol + kc:gcol + kc + 1])
            k.cp(xhi[:, kc, xcols], r[:, kc, 0:n])
            k.tt(xlo[:, kc, xcols], r[:, kc, 0:n], xhi[:, kc, xcols], ALU.subtract)

    groups = [(g * 512, 512, "p") for g in range(4)] + [(SEQ, NS, "s")]
    for (c0, n, kind) in groups:
        k.push()
        r = k.T([128, KC, n], F32, "r"); rb = k.T([128, KC, n], BF16, "rb")
        t1 = k.T([128, n], F32); t2 = k.T([128, n], F32)
        xin = k.T([128, D], F32, "xin")
        ntile = (n + 127) // 128
        for t in range(ntile):
            M = min(128, n - t * 128)
            src = xp[c0 + t * 128:c0 + t * 128 + M, :] if kind == "p" else xs[:, :]
            k.dma("sp", xin[0:M, :], src)
            for b0 in (0, 4):
                for i in range(4):
                    k.tr(PS[2 + b0 // 4][:, i * 128:i * 128 + M], xin[0:M, (b0 + i) * 128:(b0 + i + 1) * 128], identf[0:M, 0:M])
                k.cp(r[:, b0:b0 + 4, t * 128:t * 128 + M], PS[2 + b0 // 4][:, :].rearrange("p (a b) -> p a b", a=4)[:, :, 0:M], en=("act" if b0 else "dve"))
        layer_norm(r, rb, n, LIG, LIB, slice(c0, c0 + n), t1, t2)
        if mini:
            stg = k.T([128, 512], F32, "ostg")
            for t in range(ntile):
                M = min(128, n - t * 128)
                dst = yp[c0 + t * 128:c0 + t * 128 + M, :] if kind == "p" else ys[:, :]
                fm_to_tm_out([r[:, kc, t * 128:t * 128 + M] for kc in range(KC)], M, dst, stg)
        k.pop()

    ck("after input LN")
    for l in range(depth):
        last = (l == depth - 1)
        k.memset(Cst[:, :, :, :], 0.0); k.memset(Cb[:, :, :, :], 0.0)
        k.memset(uh[:, :, :], 0.0); k.memset(chh[:, :, :], 0.0); k.memset(mprev[:, :], 0.0)
        k.ts(hbg[:, :], V(l, BG, 24), 0.5)
        k.push()
        memf = k.T([128, 2, D], F32, "memf"); memT = k.T([128, KC, NMEM], BF16, "memT"); kvs = k.T([128, 512], F32, "kvs")
        k.dma("sp", memf[:, :, :], mem.rearrange("(t p) d -> p t d", p=128))
        for t in range(2):
            for b0 in (0, 4):
                for i in range(4):
                    k.tr(PS[2][:, i * 128:(i + 1) * 128], memf[:, t, (b0 + i) * 128:(b0 + i + 1) * 128], identf)
                k.cp(memT[:, b0:b0 + 4, t * 128:(t + 1) * 128], PS[2][:, :].rearrange("p (a b) -> p a b", a=4))
        for j in range(4):
            W = wnext("kv%d" % j)
            for mt in range(2):
                ps = PS[mt]
                for kc in range(KC):
                    k.mm(ps[:, :], memT[:, kc, mt * 128:(mt + 1) * 128], W[:, kc, :], start=(kc == 0), stop=(kc == KC - 1))
                k.cp(kvs[:, :], ps[:, :], en="act")
                dst = pmk if j < 2 else pmv
                k.dma("sp", dst[l, mt * 128:(mt + 1) * 128, (j % 2) * 512:(j % 2 + 1) * 512], kvs[:, :])
                if j >= 2:
                    k.cp(mvb[:, mt, (j - 2) * 512:(j - 1) * 512], ps[:, :])
            if j < 2:
                for c in range(4):
                    ps = PS[4 + c % 2]
                    for kc in range(KC):
                        k.mm(ps[:, 0:NMEM], W[:, kc, c * 128:(c + 1) * 128], memT[:, kc, :], start=(kc == 0), stop=(kc == KC - 1))
                    k.cp(mkT[:, j * 4 + c, :], ps[:, 0:NMEM])
        k.pop()
        ck("after memkv l%d" % l)

        for gi, (c0, n, kind) in enumerate(groups):
            isP = (kind == "p")
            xc = slice(c0, c0 + n)
            ntile = (n + 127) // 128
            M = min(128, n)
            k.push()
            zb = k.T([128, KC, n], BF16, "zb")

            def gate_branch(y_ps_fn, Wg, oc, gidx, first, tmpa, tmpb):
                gps = PS[1]
                proj_fm(Wg, (oc % 4) * 128, xc, n, gps)
                k.act(tmpa[:, 0:n], gps[:, 0:n], AF.Tanh, bias=hbg[:, gidx * 8 + oc:gidx * 8 + oc + 1], scale=0.5)
                yap = y_ps_fn()
                if first:
                    k.stt(zb[:, oc, :], tmpa[:, 0:n], 1.0, yap, ALU.add, ALU.mult)
                else:
                    k.stt(tmpb[:, 0:n], tmpa[:, 0:n], 1.0, yap, ALU.add, ALU.mult)
                    k.tt(zb[:, oc, :], zb[:, oc, :], tmpb[:, 0:n], ALU.add)

            k.push()
            hnT = k.T([128, KC, n], BF16, "hnT")
            gq = [k.T([4, n], F32, "gq%d" % i) for i in range(8)]
            ig, lf, mm_, bb, al, be, g1t, g2t = gq
            gint = k.T([4, n], F32, "gint"); gws = k.T([4, n], F32, "gws"); genm = k.T([4, n], F32, "genm")
            gtok = k.T([128, ntile, 16], F32, "gtok")
            W = wnext("sg")
            for kc in range(KC):
                k.mm(PS[0][0:4, 0:n], W[:, kc, 0:4], xhi[:, kc, xc], start=(kc == 0), stop=(kc == KC - 1))
            for kc in range(KC):
                k.mm(PS[1][0:4, 0:n], W[:, kc, 4:8], xhi[:, kc, xc], start=(kc == 0), stop=(kc == KC - 1))
            k.act(ig[:, :], PS[0][0:4, 0:n], AF.Identity, bias=vecs[0:4, l * VL + BIG:l * VL + BIG + 1])
            k.act(g1t[:, :], PS[1][0:4, 0:n], AF.Identity, bias=vecs[0:4, l * VL + BFG:l * VL + BFG + 1])
            k.act(g2t[:, :], g1t[:, :], AF.Abs)
            k.act(g2t[:, :], g2t[:, :], AF.Exp, scale=-1.0)
            k.act(g2t[:, :], g2t[:, :], AF.Ln, bias=1.0)
            k.ts(g1t[:, :], g1t[:, :], 0.0, None, ALU.min)
            k.tt(lf[:, :], g1t[:, :], g2t[:, :], ALU.subtract)
            if isP:
                k.g("dve", "tensor_tensor_scan", out=mm_[:, :], data0=lf[:, :], data1=ig[:, :], initial=mprev[:, 0:1], op0=ALU.add, op1=ALU.max)
                k.g("dve", "tensor_tensor_scan", out=bb[:, :], data0=cst[0:4, C_RMASK:C_RMASK + n], data1=lf[:, :], initial=0.0, op0=ALU.mult, op1=ALU.add)
                k.tt(be[:, :], bb[:, :], mm_[:, :], ALU.subtract)
                k.tt(al[:, :], ig[:, :], bb[:, :], ALU.subtract)
                mpc = k.T([4, 4], F32, "mpc"); blc = k.T([4, 4], F32, "blc")
                k.cp(mpc[:, 0:1], mprev[:, 0:1])
                k.cp(mpc[:, 1:4], mm_[:, 127:383 + 1:128])
                k.cp(blc[:, :], be[:, 127:n:128])
                k.tt(g1t[:, :].rearrange("p (c t) -> p c t", c=4), be[:, :].rearrange("p (c t) -> p c t", c=4), mpc[:, :].unsqueeze(2).broadcast_to([4, 4, 128]), ALU.add)
                k.act(gint[:, :], g1t[:, :], AF.Exp)
                k.tt(g2t[:, :].rearrange("p (c t) -> p c t", c=4), al[:, :].rearrange("p (c t) -> p c t", c=4), blc[:, :].unsqueeze(2).broadcast_to([4, 4, 128]), ALU.add)
                k.act(gws[:, :], g2t[:, :], AF.Exp, bias=-LN16)
                k.act(genm[:, :], mm_[:, :], AF.Exp, scale=-1.0)
                k.cp(mprev[:, 0:1], mm_[:, n - 1:n])
                if gi == 3:
                    k.dma("sp", pm[l, :].rearrange("(h o) -> h o", o=1), mm_[:, n - 1:n], slow=True)
            else:
                m0 = k.T([4, NS], F32, "m0")
                k.dma("sp", m0[:, :], sm[l, :, :].rearrange("j h -> h j"), slow=True)
                k.tt(g1t[:, :], lf[:, :], m0[:, :], ALU.add)
                k.tt(mm_[:, :], g1t[:, :], ig[:, :], ALU.max)
                k.tt(g1t[:, :], g1t[:, :], mm_[:, :], ALU.subtract)
                k.act(gint[:, :], g1t[:, :], AF.Exp)
                k.tt(g2t[:, :], ig[:, :], mm_[:, :], ALU.subtract)
                k.act(gws[:, :], g2t[:, :], AF.Exp, bias=-LN16)
                k.act(genm[:, :], mm_[:, :], AF.Exp, scale=-1.0)
            for t in range(ntile):
                for qi, qt in enumerate((gint, gws, genm, mm_)):
                    k.tr(PS[3][0:M, qi * 4:(qi + 1) * 4], qt[:, t * 128:t * 128 + M], identf[0:4, 0:4])
                k.cp(gtok[0:M, t, :], PS[3][0:M, 0:16], en="act")
            if not isP:
                k.dma("sp", smo[l, :, :], gtok[0:NS, 0, 12:16])

            qT = k.T([128, 2, n], BF16, "qT"); kT = k.T([128, 2, n], BF16, "kT"); qs = k.T([128, 2, n], BF16, "qs")
            vaug = k.T([128, ntile, 257], BF16, "vaug"); sigo = k.T([128, ntile, 256], F32, "sigo")
            kw = k.T([128, ntile, 256], BF16, "kw"); hc = k.T([128, ntile, 256], F32, "hc")
            bvo = k.T([128, 512], F32, "bvo"); decb = k.T([128, max(4, n if not isP else 4)], F32, "decb")
            Dt = k.T([128, 128], F32, "Dt"); SD = k.T([128, 128], BF16, "SD"); hn = k.T([128, 256], BF16, "hn")
            st8 = k.T([128, 8 * ntile], F32, "st8"); junk = k.T([128, 256], F32, "junk")
            k.memset(vaug[:, :, 256:257], 1.0)
            if not isP:
                bqk = k.T([NS, 512], F32, "bqk"); qk_tok = k.T([NS, 512], F32, "qktok"); kwf = k.T([NS, 256], F32, "kwf")
                Qm = k.T([128, 2, NS, NS], BF16, "Qm"); kwm = k.T([NS, 256], BF16, "kwm")
                Cs = [k.T([128, 2, 256], F32, "Cs%d" % i) for i in range(2)]
                Cn = [k.T([128, 2, 256], F32, "Cn%d" % i) for i in range(2)]
                Cnb = k.T([128, 2, 256], BF16, "Cnb")
                n0 = k.T([NS, H * DH], F32, "n0"); nn = k.T([NS, H * DH], F32, "nn"); vtok = k.T([NS, 256], BF16, "vtok")
                k.dma("sp", n0[:, :], sn[l, :, :])
            for h in range(H):
                WA = wnext("A%d" % h)
                WB = wnext("B%d" % h)
                k.dma("sp", bvo[0:M, 0:256], b_in[l:l + 1, OV + h * 256:OV + (h + 1) * 256].broadcast_to([M, 256]))
                k.dma("sp", bvo[0:M, 256:512], b_in[l:l + 1, OO + h * 256:OO + (h + 1) * 256].broadcast_to([M, 256]))
                for dc in range(2):
                    proj_fm(WA, dc * 128, xc, n, PS[0])
                    k.act(qT[:, dc, :], PS[0][:, 0:n], AF.Identity, bias=V(l, BQ + h * 2 + dc))
                    if isP:
                        proj_fm(WA, 256 + dc * 128, xc, n, PS[1])
                        k.ts(kT[:, dc, :], PS[1][:, 0:n], V(l, BK + h * 2 + dc), None, ALU.add)
                for t in range(ntile):
                    tc_ = slice(c0 + t * 128, c0 + t * 128 + M)
                    for kc in range(KC):
                        k.mm(PS[2][0:M, :], xhi[:, kc, tc_], WB[:, kc, :], start=(kc == 0), stop=(kc == KC - 1))
                    k.tt(vaug[0:M, t, 0:256], PS[2][0:M, 0:256], bvo[0:M, 0:256], ALU.add)
                    k.tt(sigo[0:M, t, :], PS[2][0:M, 256:512], bvo[0:M, 256:512], ALU.add)
                    k.act(sigo[0:M, t, :], sigo[0:M, t, :], AF.Tanh, scale=0.5)
                    k.ts(sigo[0:M, t, :], sigo[0:M, t, :], 0.5, 0.5, ALU.mult, ALU.add)
                if isP:
                    for t in range(ntile):
                        for dc in range(2):
                            k.tr(PSB[3][:, dc * 128:(dc + 1) * 128], kT[:, dc, t * 128:(t + 1) * 128], identb[:, :])
                        k.ts(kw[:, t, :], PSB[3][:, 0:256], gtok[:, t, 4 + h:5 + h], None, ALU.mult)
                    k.mm(PS[2][:, 0:n], selh(h), gint[:, :], start=True, stop=True)
                    k.cp(decb[:, 0:4], PS[2][:, 127:n:128])
                    for dc in range(2):
                        k.tt(qs[:, dc, :], PS[2][:, 0:n], qT[:, dc, :], ALU.mult)
                    for c in range(4):
                        cc = slice(c * 128, (c + 1) * 128)
                        k.mm(PS[4][:, 0:128], selh(h), be[:, cc], start=True, stop=False)
                        k.mm(PS[4][:, 0:128], al[:, cc], selh(h), start=False, stop=False)
                        k.mm(PS[4][:, 0:128], identf, negm, start=False, stop=True)
                        k.act(Dt[:, :], PS[4][:, 0:128], AF.Exp, bias=-LN16)
                        for dc in range(2):
                            k.mm(PS[1][:, 0:128], kT[:, dc, cc], qT[:, dc, cc], start=(dc == 0), stop=(dc == 1))
                        k.tt(SD[:, :], PS[1][:, 0:128], Dt[:, :], ALU.mult)
                        k.mm(PS[5][:, 0:257], SD[:, :], vaug[:, c, :], start=True, stop=False)
                        k.mm(PS[5][:, 0:257], qs[:, 0, cc], Cb[:, h, 0, :], start=False, stop=False)
                        k.mm(PS[5][:, 0:257], qs[:, 1, cc], Cb[:, h, 1, :], start=False, stop=True)
                        k.mm(PS[6][:, 0:257], kw[:, c, 0:128], vaug[:, c, :], start=True, stop=True)
                        k.mm(PS[7][:, 0:257], kw[:, c, 128:256], vaug[:, c, :], start=True, stop=True)
                        k.stt(Cst[:, h, 0, :], Cst[:, h, 0, :], decb[:, c:c + 1], PS[6][:, 0:257], ALU.mult, ALU.add)
                        k.stt(Cst[:, h, 1, :], Cst[:, h, 1, :], decb[:, c:c + 1], PS[7][:, 0:257], ALU.mult, ALU.add)
                        k.cp(Cb[:, h, :, :], Cst[:, h, :, :], en="act")
                        k.act(st8[:, c * 8:c * 8 + 1], PS[5][:, 256:257], AF.Abs)
                        k.tt(st8[:, c * 8:c * 8 + 1], st8[:, c * 8:c * 8 + 1], gtok[:, c, 8 + h:9 + h], ALU.max)
                        k.recip(st8[:, c * 8 + 1:c * 8 + 2], st8[:, c * 8:c * 8 + 1])
                        k.stt(hc[:, c, :], PS[5][:, 0:256], st8[:, c * 8 + 1:c * 8 + 2], sigo[:, c, :], ALU.mult, ALU.mult)
                    if gi == 3:
                        k.dma("sp", pC[l, h, :, :].rearrange("(dc p) e -> p dc e", p=128), Cst[:, h, :, 0:256])
                        k.dma("sp", pn[l, h * DH:(h + 1) * DH].rearrange("(dc p o) -> p dc o", p=128, o=1), Cst[:, h, :, 256:257], slow=True)
                else:
                    k.dma("sp", bqk[:, 0:256], b_in[l:l + 1, OQ + h * 256:OQ + (h + 1) * 256].broadcast_to([NS, 256]))
                    k.dma("sp", bqk[:, 256:512], b_in[l:l + 1, OK_ + h * 256:OK_ + (h + 1) * 256].broadcast_to([NS, 256]))
                    for kc in range(KC):
                        k.mm(PS[1][0:NS, :], xhi[:, kc, xc], WA[:, kc, :], start=(kc == 0), stop=(kc == KC - 1))
                    k.tt(qk_tok[:, :], PS[1][0:NS, :], bqk[:, :], ALU.add)
                    k.ts(kwf[:, :], qk_tok[:, 256:512], gtok[0:NS, 0, 4 + h:5 + h], None, ALU.mult)
                    k.cp(vtok[:, :], vaug[0:NS, 0, 0:256])
                    k.stt(nn[:, h * DH:(h + 1) * DH], n0[:, h * DH:(h + 1) * DH], gtok[0:NS, 0, h:h + 1], kwf[:, :], ALU.mult, ALU.add)
                    k.mm(PS[2][:, 0:NS], selh(h), gint[:, :], start=True, stop=True)
                    k.cp(decb[:, 0:NS], PS[2][:, 0:NS])
                    for dc in range(2):
                        k.tt(Qm[:, dc, :, :], qT[:, dc, :].unsqueeze(2).broadcast_to([128, NS, NS]),
                             cst[:, C_DIAG:C_DIAG + 256].rearrange("p (a b) -> p a b", a=NS), ALU.mult)
                    for j in range(NS):
                        cs = Cs[j % 2]; cn = Cn[j % 2]
                        k.dma("sp", cs[:, :, :], sC[l, j, h, :, :].rearrange("(dc p) e -> p dc e", p=128))
                        k.ts(kwm[:, :], kwf[:, :], identf[0:NS, j:j + 1], None, ALU.mult)
                        for dc in range(2):
                            k.mm(PS[6 + dc][:, 0:256], kwm[:, dc * 128:(dc + 1) * 128], vtok[:, :], start=True, stop=True)
                            k.stt(cn[:, dc, :], cs[:, dc, :], decb[:, j:j + 1], PS[6 + dc][:, 0:256], ALU.mult, ALU.add)
                        k.dma("sp", sCo[l, j, h, :, :].rearrange("(dc p) e -> p dc e", p=128), cn[:, :, :])
                        k.cp(Cnb[:, :, :], cn[:, :, :], en="act")
                        for dc in range(2):
                            k.mm(PS[5][0:NS, 0:256], Qm[:, dc, j, :], Cnb[:, dc, :], start=(j == 0 and dc == 0), stop=(j == NS - 1 and dc == 1))
                    k.tt(junk[0:NS, :], qk_tok[:, 0:256], nn[:, h * DH:(h + 1) * DH], ALU.mult)
                    k.red(st8[0:NS, 2:3], junk[0:NS, :], ALU.add)
                    k.act(st8[0:NS, 0:1], st8[0:NS, 2:3], AF.Abs)
                    k.tt(st8[0:NS, 0:1], st8[0:NS, 0:1], gtok[0:NS, 0, 8 + h:9 + h], ALU.max)
                    k.recip(st8[0:NS, 1:2], st8[0:NS, 0:1])
                    k.stt(hc[0:NS, 0, :], PS[5][0:NS, 0:256], st8[0:NS, 1:2], sigo[0:NS, 0, :], ALU.mult, ALU.mult)
                    if h == H - 1:
                        k.dma("sp", sno[l, :, :], nn[:, :])
                for t in range(ntile):
                    o = t * 8
                    k.red(st8[0:M, o + 2:o + 3], hc[0:M, t, :], ALU.add)
                    k.ts(st8[0:M, o + 3:o + 4], st8[0:M, o + 2:o + 3], -1.0 / DH)
                    k.act(junk[0:M, :], hc[0:M, t, :], AF.Square, bias=st8[0:M, o + 3:o + 4], accum_out=st8[0:M, o + 4:o + 5])
                    k.ts(st8[0:M, o + 5:o + 6], st8[0:M, o + 4:o + 5], 1.0 / DH, EPS, ALU.mult, ALU.add)
                    k.act(st8[0:M, o + 5:o + 6], st8[0:M, o + 5:o + 6], AF.Ln)
                    k.act(st8[0:M, o + 5:o + 6], st8[0:M, o + 5:o + 6], AF.Exp, scale=-0.5)
                    k.ts(hn[0:M, :], hc[0:M, t, :], st8[0:M, o + 3:o + 4], st8[0:M, o + 5:o + 6], ALU.add, ALU.mult)
                    for dc in range(2):
                        k.tr(PSB[3][:, dc * 128:dc * 128 + M], hn[0:M, dc * 128:(dc + 1) * 128], identb[0:M, 0:M])
                        k.ts(hnT[:, h * 2 + dc, t * 128:t * 128 + M], PSB[3][:, dc * 128:dc * 128 + M], V(l, MHG + h * 2 + dc), None, ALU.mult)
            ta = k.T([128, n], F32, "ta"); tb = k.T([128, n], F32, "tb")
            for j in range(2):
                Wm = wnext("wm%d" % j); Wg = wnext("g0_%d" % j)
                for o4 in range(4):
                    oc = j * 4 + o4

                    def yfn(Wm=Wm, o4=o4):
                        for kc in range(KC):
                            k.mm(PS[0][:, 0:n], Wm[:, kc, o4 * 128:(o4 + 1) * 128], hnT[:, kc, :], start=(kc == 0), stop=(kc == KC - 1))
                        return PS[0][:, 0:n]
                    gate_branch(yfn, Wg, oc, 0, True, ta, tb)
            k.pop()
            ck("after A l%d g%d" % (l, gi))

            k.push()
            dT = k.T([128, KC, n], BF16, "dT")
            ub = k.T([128, 16 + n], F32, "ub"); s2 = k.T([128, 16 + n], F32, "s2"); s4 = k.T([128, 16 + n], F32, "s4")
            ta = k.T([128, n], F32, "ta"); tb = k.T([128, n], F32, "tb")
            if not isP:
                us = k.T([128, KC, NS], F32, "us"); utok = k.T([NS, D], F32, "utok"); dtok = k.T([NS, D], F32, "dtok")
                spl = [k.T([120, D], F32, "spl%d" % i) for i in range(2)]
                for hf in range(2):
                    k.dma("sp", spl[hf][:, :], spool[l, hf * 8:(hf + 1) * 8, :, :].rearrange("j r d -> (j r) d"))
                k.dma("sp", spoolo[l, :, 0:14, :], spool[l, :, 1:15, :])
            for j in range(2):
                W = wnext("u%d" % j)
                for c4 in range(4):
                    ch = j * 4 + c4
                    g_ = ch // 2
                    w = 2 ** (g_ + 1)
                    proj_fm(W, c4 * 128, xc, n, PS[0])
                    if isP:
                        k.cp(ub[:, 0:16], uh[:, ch, :])
                        k.act(ub[:, 16:16 + n], PS[0][:, 0:n], AF.Identity, bias=V(l, BU + ch))
                        k.cp(uh[:, ch, :], ub[:, n:n + 16])
                        L = 16 + n
                        k.tt(s2[:, 1:L], ub[:, 1:L], ub[:, 0:L - 1], ALU.add)
                        cur = s2
                        if w >= 4:
                            k.tt(s4[:, 3:L], s2[:, 3:L], s2[:, 1:L - 2], ALU.add); cur = s4
                        if w >= 8:
                            k.tt(s2[:, 7:L], s4[:, 7:L], s4[:, 3:L - 4], ALU.add); cur = s2
                        if w >= 16:
                            k.tt(s4[:, 15:L], s2[:, 15:L], s2[:, 7:L - 8], ALU.add); cur = s4
                        k.stt(dT[:, ch, :], cur[:, 16:L], 1.0 / w, ub[:, 16:L], ALU.mult, ALU.subtract)
                        if gi == 0 and w > 2:
                            k.tt(ta[:, 0:w - 1], cur[:, 16:16 + w - 1], cst[:, C_RC:C_RC + w - 1], ALU.mult)
                            k.tt(dT[:, ch, 0:w - 1], ta[:, 0:w - 1], ub[:, 16:16 + w - 1], ALU.subtract)
                        elif gi == 0:
                            k.tt(ta[:, 0:1], cur[:, 16:17], cst[:, C_RC:C_RC + 1], ALU.mult)
                            k.tt(dT[:, ch, 0:1], ta[:, 0:1], ub[:, 16:17], ALU.subtract)
                    else:
                        k.act(us[:, ch, :], PS[0][:, 0:n], AF.Identity, bias=V(l, BU + ch))
            if isP and gi == 3:
                stg = k.T([16, 512], F32, "stg")
                fm_to_tm_out([uh[:, ch, :] for ch in range(KC)], 16, ppool[l, :, :], stg, row0=1)
            if not isP:
                for b0 in (0, 4):
                    for i in range(4):
                        k.tr(PS[4 + b0 // 4][0:NS, i * 128:(i + 1) * 128], us[:, b0 + i, :], identf)
                    k.cp(utok[:, b0 * 128:(b0 + 4) * 128], PS[4 + b0 // 4][0:NS, :])
                k.dma("sp", spoolo[l, :, 14, :], utok[:, :])
                for g_ in range(4):
                    w = 2 ** (g_ + 1)
                    ps = PS[6][0:NS, g_ * 128:g_ * 128 + 128] if False else None
                    pso = PS[6 + g_ // 2]
                    cs_ = slice((g_ % 2) * 256, (g_ % 2) * 256 + 256)
                    for hf in range(2):
                        k.mm(pso[0:NS, cs_], cst[0:120, C_SELP + (hf * 4 + g_) * 16:C_SELP + (hf * 4 + g_ + 1) * 16], spl[hf][:, g_ * 256:(g_ + 1) * 256], start=(hf == 0), stop=(hf == 1))
                    k.ts(dtok[:, g_ * 256:(g_ + 1) * 256], utok[:, g_ * 256:(g_ + 1) * 256], 1.0 / w - 1.0)
                    k.stt(dtok[:, g_ * 256:(g_ + 1) * 256], pso[0:NS, cs_], 1.0 / w, dtok[:, g_ * 256:(g_ + 1) * 256], ALU.mult, ALU.add)
                for b0 in (0, 4):
                    for i in range(4):
                        k.tr(PS[4 + b0 // 4][:, i * 128:i * 128 + NS], dtok[:, (b0 + i) * 128:(b0 + i + 1) * 128], identf[0:NS, 0:NS])
                    k.cp(dT[:, b0:b0 + 4, :], PS[4 + b0 // 4][:, :].rearrange("p (a b) -> p a b", a=4)[:, :, 0:NS])
            WPr = wnext("wp")
            WP = k.T([128, KC, 256], BF16, "wps")
            k.cp(WP[:, :, :], WPr[:, :, 0:256])
            ysb = k.T([128, n], F32, "ysb")
            for j in range(2):
                Wg = wnext("g1_%d" % j)
                for o4 in range(4):
                    oc = j * 4 + o4
                    g_ = oc // 2; e = oc % 2

                    def yfn(g_=g_, e=e, oc=oc):
                        for c2 in range(2):
                            k.mm(PS[0][:, 0:n], WP[:, g_ * 2 + c2, e * 128:(e + 1) * 128], dT[:, g_ * 2 + c2, :], start=(c2 == 0), stop=(c2 == 1))
                        k.act(ysb[:, 0:n], PS[0][:, 0:n], AF.Identity, scale=V(l, PSC + oc))
                        return ysb[:, 0:n]
                    gate_branch(yfn, Wg, oc, 1, False, ta, tb)
            k.pop()
            ck("after B l%d g%d" % (l, gi))

            k.push()
            xqT = k.T([128, KC, n], BF16, "xqT"); oxT = k.T([128, KC, n], BF16, "oxT")
            ta = k.T([128, n], F32, "ta"); tb = k.T([128, n], F32, "tb")
            for j in range(2):
                W = wnext("xq%d" % j)
                for c4 in range(4):
                    ch = j * 4 + c4
                    proj_fm(W, c4 * 128, xc, n, PS[c4 % 2])
                    if c4 % 2:
                        k.ts(xqT[:, ch, :], PS[1][:, 0:n], V(l, BXQ + ch), None, ALU.add)
                    else:
                        k.act(xqT[:, ch, :], PS[0][:, 0:n], AF.Identity, bias=V(l, BXQ + ch))
            if isP:
                Pf = k.T([128, H, NMEM], F32, "Pf"); Pn = k.T([128, H, NMEM], BF16, "Pn"); PT = k.T([128, H, 2, 128], BF16, "PT")
                sx = k.T([128, 16], F32, "sx")
                for t in range(ntile):
                    tcs = slice(t * 128, (t + 1) * 128)
                    for h in range(H):
                        ps = PS[4 + h // 2]
                        for dc in range(2):
                            k.mm(ps[:, (h % 2) * 256:(h % 2 + 1) * 256], xqT[:, h * 2 + dc, tcs], mkT[:, h * 2 + dc, :], start=(dc == 0), stop=(dc == 1))
                    for hp in range(2):
                        k.red(sx[:, hp * 2:hp * 2 + 2], PS[4 + hp][:, :].rearrange("p (a b) -> p a b", a=2), ALU.max)
                    k.ts(sx[:, 4:8], sx[:, 0:4], -1.0 / 16.0)
                    for h in range(H):
                        k.act(Pf[:, h, :], PS[4 + h // 2][:, (h % 2) * 256:(h % 2 + 1) * 256], AF.Exp, bias=sx[:, 4 + h:5 + h], scale=1.0 / 16.0, accum_out=sx[:, 8 + h:9 + h])
                    k.recip(sx[:, 12:16], sx[:, 8:12])
                    for h in range(H):
                        k.ts(Pn[:, h, :], Pf[:, h, :], sx[:, 12 + h:13 + h], None, ALU.mult)
                        for mc in range(2):
                            k.tr(PSB[3][:, mc * 128:(mc + 1) * 128], Pn[:, h, mc * 128:(mc + 1) * 128], identb[:, :])
                        k.cp(PT[:, h, :, :], PSB[3][:, 0:256].rearrange("p (a b) -> p a b", a=2), en="act")
                    for h in range(H):
                        for dc in range(2):
                            ps = PS[6 + dc]
                            for mc in range(2):
                                k.mm(ps[:, 0:128], mvb[:, mc, h * 256 + dc * 128:h * 256 + (dc + 1) * 128], PT[:, h, mc, :], start=(mc == 0), stop=(mc == 1))
                            k.cp(oxT[:, h * 2 + dc, tcs], ps[:, 0:128], en=("act" if dc else "dve"))
            else:
                xqtok = k.T([NS, D], F32, "xqtok"); selj = k.T([NS, 128], BF16, "selj"); xqtb = k.T([NS, D], BF16, "xqtb")
                mkb = [k.T([128, 2, D], F32, "mkb%d" % i) for i in range(2)]
                prod = k.T([128, 2, D], F32, "prod"); sall = k.T([128, 2, NS * H], F32, "sall")
                sT = k.T([64, NMEM], F32, "sT"); pT2 = k.T([128, 2, NS * H], F32, "pT2")
                Pm = k.T([128, 2, NS, H, NS], BF16, "Pm"); mvs = k.T([128, 2, D], BF16, "mvs")
                otok = k.T([NS, D], BF16, "otok"); sx = k.T([64, 8], F32, "sx")
                for b0 in (0, 4):
                    for i in range(4):
                        k.tr(PSB[4 + b0 // 4][0:NS, i * 128:(i + 1) * 128], xqT[:, b0 + i, :], identb[:, :])
                    k.cp(xqtb[:, b0 * 128:(b0 + 4) * 128], PSB[4 + b0 // 4][0:NS, 0:512])
                for j in range(NS):
                    mk_ = mkb[j % 2]
                    k.dma("sp", mk_[:, :, :], cmk[l, j, :, :].rearrange("(mc p) d -> p mc d", p=128))
                    k.ts(selj[:, :], cst[0:NS, C_ONES:C_ONES + 128], identf[0:NS, j:j + 1], None, ALU.mult)
                    for hf in range(2):
                        k.mm(PS[hf][:, :], selj[:, :], xqtb[:, hf * 512:(hf + 1) * 512], start=True, stop=True)
                    for hf in range(2):
                        k.tt(prod[:, :, hf * 512:(hf + 1) * 512], mk_[:, :, hf * 512:(hf + 1) * 512], PS[hf][:, :].unsqueeze(1).broadcast_to([128, 2, 512]), ALU.mult)
                    k.red(sall[:, :, j * H:(j + 1) * H], prod[:, :, :].rearrange("p m (h d) -> p m h d", h=H), ALU.add)
                for mc in range(2):
                    k.tr(PS[4][0:64, mc * 128:(mc + 1) * 128], sall[:, mc, :], identf)
                k.red(sx[:, 0:1], PS[4][0:64, 0:NMEM], ALU.max)
                k.ts(sx[:, 1:2], sx[:, 0:1], -1.0 / 16.0)
                k.act(sT[:, :], PS[4][0:64, 0:NMEM], AF.Exp, bias=sx[:, 1:2], scale=1.0 / 16.0, accum_out=sx[:, 2:3])
                k.recip(sx[:, 3:4], sx[:, 2:3])
                k.ts(sT[:, :], sT[:, :], sx[:, 3:4], None, ALU.mult)
                for mc in range(2):
                    k.tr(PS[5][:, mc * 64:(mc + 1) * 64], sT[:, mc * 128:(mc + 1) * 128], identf[0:64, 0:64])
                k.cp(pT2[:, :, :], PS[5][:, 0:128].rearrange("p (a b) -> p a b", a=2))
                for mc in range(2):
                    k.tt(Pm[:, mc, :, :, :], pT2[:, mc, :].rearrange("p (j h) -> p j h", j=NS).unsqueeze(3).broadcast_to([128, NS, H, NS]),
                         cst[:, C_DIAG:C_DIAG + 256].rearrange("p (a b) -> p a b", a=NS).unsqueeze(2).broadcast_to([128, NS, H, NS]), ALU.mult)
                for j in range(NS):
                    mv_ = mkb[j % 2]
                    k.dma("sp", mv_[:, :, :], cmv[l, j, :, :].rearrange("(mc p) d -> p mc d", p=128))
                    k.cp(mvs[:, :, :], mv_[:, :, :], en="act")
                    for h in range(H):
                        for mc in range(2):
                            first = (j == 0 and mc == 0 and h % 2 == 0)
                            k.nc
                            k.op("pe", (lambda h=h, mc=mc, j=j, first=first: nc.tensor.matmul(
                                PS[6 + h // 2][0:NS, (h % 2) * 256:(h % 2 + 1) * 256], Pm[:, mc, j, h, :], mvs[:, mc, h * 256:(h + 1) * 256],
                                start=first, stop=(j == NS - 1 and mc == 1), skip_group_check=True)),
                                [Pm[:, mc, j, h, :], mvs[:, mc, h * 256:(h + 1) * 256]], [PS[6 + h // 2][0:NS, (h % 2) * 256:(h % 2 + 1) * 256]])
                for hp in range(2):
                    k.cp(otok[:, hp * 512:(hp + 1) * 512], PS[6 + hp][0:NS, :])
                for b0 in (0, 4):
                    for i in range(4):
                        k.tr(PSB[4 + b0 // 4][:, i * 128:i * 128 + NS], otok[:, (b0 + i) * 128:(b0 + i + 1) * 128], identb[0:NS, 0:NS])
                    k.cp(oxT[:, b0:b0 + 4, :], PSB[4 + b0 // 4][:, 0:512].rearrange("p (a b) -> p a b", a=4)[:, :, 0:NS])
            for j in range(2):
                Wx = wnext("wx%d" % j); Wg = wnext("g2_%d" % j)
                for o4 in range(4):
                    oc = j * 4 + o4

                    def yfn(Wx=Wx, o4=o4):
                        for kc in range(KC):
                            k.mm(PS[0][:, 0:n], Wx[:, kc, o4 * 128:(o4 + 1) * 128], oxT[:, kc, :], start=(kc == 0), stop=(kc == KC - 1))
                        return PS[0][:, 0:n]
                    gate_branch(yfn, Wg, oc, 2, False, ta, tb)
            k.pop()
            ck("after C l%d g%d" % (l, gi))

            k.push()
            r = k.T([128, KC, n], F32, "r"); rb = k.T([128, KC, n], BF16, "rb")
            t1 = k.T([128, n], F32); t2 = k.T([128, n], F32)
            k.act(zb[:, :, :], zb[:, :, :], AF.Copy, scale=0.5)
            for j in range(2):
                W = wnext("wo%d" % j)
                for o4 in range(4):
                    oc = j * 4 + o4
                    ps = PS[oc % 2]
                    for kc in range(KC):
                        k.mm(ps[:, 0:n], W[:, kc, o4 * 128:(o4 + 1) * 128], zb[:, kc, :], start=(kc == 0), stop=(kc == KC - 1))
                    k.tt(t1[:, 0:n], xhi[:, oc, xc], xlo[:, oc, xc], ALU.add)
                    k.stt(r[:, oc, :], t1[:, 0:n], ALPHA, ps[:, 0:n], ALU.mult, ALU.add)
            layer_norm(r, rb, n, l * VL + L1G, l * VL + L1B, xc, t1, t2)
            k.pop()
            ck("after D l%d g%d" % (l, gi))

            k.push()
            actT = k.T([128, FC, n], BF16, "actT")
            r = k.T([128, KC, n], F32, "r"); rb = k.T([128, KC, n], BF16, "rb")
            t1 = k.T([128, n], F32); t2 = k.T([128, n], F32)
            ca = k.T([128, n], F32, "ca"); cbt = k.T([128, n], F32, "cbt"); gt_ = k.T([128, n], F32, "gt")
            if not isP:
                bufT = k.T([128, 2 * FC, 2 * NS], F32, "bufT"); cstg = k.T([2 * NS, 1408], F32, "cstg"); hsk = k.T([128, 2 * FC, NS], F32, "hsk")
                k.dma("sp", sconvo[l, :, 0, :], sconv[l, :, 1, :])
                for pc in range(4):
                    k.dma("sp", cstg[:, :], sconv[l, :, :, pc * 1408:(pc + 1) * 1408].rearrange("j r c -> (j r) c"))
                    for i in range(11):
                        k.tr(PS[4 + i % 2][:, 0:2 * NS], cstg[:, i * 128:(i + 1) * 128], identf[0:2 * NS, 0:2 * NS])
                        k.cp(bufT[:, pc * 11 + i, :], PS[4 + i % 2][:, 0:2 * NS], en=("act" if i % 2 else "dve"))
            for j in range(11):
                W = wnext("up%d" % j)
                for i in range(2):
                    cidx = j * 2 + i
                    res = []
                    for half, (ct, ps) in enumerate(((ca, PS[0]), (cbt, PS[1]))):
                        chn = cidx + half * FC
                        proj_fm(W, half * 256 + i * 128, xc, n, ps)
                        w0 = V(l, CW + 0 * 44 + chn); w1 = V(l, CW + 1 * 44 + chn); w2 = V(l, CW + 2 * 44 + chn); cbv = V(l, CB + chn)
                        k.act(ct[:, 0:n], ps[:, 0:n], AF.Identity, bias=cbv, scale=w2)
                        if isP:
                            k.stt(ct[:, 1:n], ps[:, 0:n - 1], w1, ct[:, 1:n], ALU.mult, ALU.add)
                            k.stt(ct[:, 2:n], ps[:, 0:n - 2], w0, ct[:, 2:n], ALU.mult, ALU.add)
                            k.stt(ct[:, 0:1], chh[:, chn, 1:2], w1, ct[:, 0:1], ALU.mult, ALU.add)
                            k.stt(ct[:, 0:2], chh[:, chn, 0:2], w0, ct[:, 0:2], ALU.mult, ALU.add)
                            k.cp(chh[:, chn, :], ps[:, n - 2:n])
                        else:
                            bv = bufT[:, chn, :].rearrange("p (j r) -> p r j", r=2)
                            k.stt(ct[:, 0:n], bv[:, 1, :], w1, ct[:, 0:n], ALU.mult, ALU.add)
                            k.stt(ct[:, 0:n], bv[:, 0, :], w0, ct[:, 0:n], ALU.mult, ALU.add)
                            k.cp(hsk[:, chn, :], ps[:, 0:n], en="act")
                    k.act(gt_[:, 0:n], ca[:, 0:n], AF.Square)
                    k.ts(gt_[:, 0:n], gt_[:, 0:n], 0.044715, 1.0, ALU.mult, ALU.add)
                    k.tt(gt_[:, 0:n], gt_[:, 0:n], ca[:, 0:n], ALU.mult)
                    k.act(gt_[:, 0:n], gt_[:, 0:n], AF.Tanh, scale=GK)
                    k.stt(gt_[:, 0:n], gt_[:, 0:n], 1.0, ca[:, 0:n], ALU.add, ALU.mult)
                    k.stt(actT[:, cidx, :], gt_[:, 0:n], 0.5, cbt[:, 0:n], ALU.mult, ALU.mult)
            if isP and gi == 3:
                stg = k.T([16, 512], F32, "stg")
                fm_to_tm_out([chh[:, c, :] for c in range(2 * FC)], 2, pconv[l, :, :], stg)
            if not isP:
                stg = k.T([16, 512], F32, "stg")
                fm_to_tm_out([hsk[:, c, :] for c in range(2 * FC)], NS, sconvo[l, :, 1, :], stg)
            for cr in range(2):
                for kr in range(3):
                    W = wnext("dn%d%d" % (cr, kr))
                    nk = 8 if kr < 2 else 6
                    for o4 in range(4):
                        for kc in range(nk):
                            k.mm(PS[4 + o4][:, 0:n], W[:, kc, o4 * 128:(o4 + 1) * 128], actT[:, kr * 8 + kc, :], start=(kr == 0 and kc == 0), stop=(kr == 2 and kc == nk - 1))
                for o4 in range(4):
                    oc = cr * 4 + o4
                    k.tt(t1[:, 0:n], xhi[:, oc, xc], xlo[:, oc, xc], ALU.add)
                    k.stt(r[:, oc, :], t1[:, 0:n], ALPHA, PS[4 + o4][:, 0:n], ALU.mult, ALU.add)
            layer_norm(r, rb, n, l * VL + L2G, l * VL + L2B, xc, t1, t2)
            if last:
                stg = k.T([128, 512], F32, "ostg")
                for t in range(ntile):
                    dst = yp[c0 + t * 128:c0 + t * 128 + M, :] if isP else ys[:, :]
                    fm_to_tm_out([r[:, kc, t * 128:t * 128 + M] for kc in range(KC)], M, dst, stg)
            k.pop()
            ck("after E l%d g%d" % (l, gi))
            k.pop()
    k.barrier(final=True)
    assert st["wi"] == len(sched)


def _consts():
    c = np.zeros((128, NCST), np.float32)
    c[:, C_ID:C_ID + 128] = np.eye(128, dtype=np.float32)
    s_, t_ = np.meshgrid(np.arange(128), np.arange(128), indexing="ij")
    c[:, C_NEGM:C_NEGM + 128] = np.where(s_ <= t_, 0.0, NEG)
    for h in range(4):
        c[h, C_SELH + h * 128:C_SELH + (h + 1) * 128] = 1.0
    rm = np.ones(512, np.float32); rm[::128] = 0.0
    c[0:4, C_RMASK:C_RMASK + 512] = rm
    c[:, C_DIAG:C_DIAG + 256] = np.eye(16, dtype=np.float32).reshape(-1)
    c[:, C_RC:C_RC + 16] = 1.0 / (np.arange(16) + 1.0)
    for hf in range(2):
        for g in range(4):
            w = 2 ** (g + 1)
            for j in range(8):
                for r in range(15):
                    if r >= 16 - w:
                        c[j * 15 + r, C_SELP + (hf * 4 + g) * 16 + hf * 8 + j] = 1.0
    c[:, C_ONES:C_ONES + 128] = 1.0
    return c


def _vecs(inp):
    v = np.zeros((128, NV), np.float32)

    def fm(a):
        return np.ascontiguousarray(a.reshape(-1, 128).T)
    for l in range(DEPTH):
        o = l * VL
        b = inp["b_in"][l]
        v[:, o + BQ:o + BQ + 8] = fm(b[OQ:OQ + 1024]); v[:, o + BK:o + BK + 8] = fm(b[OK_:OK_ + 1024])
        v[:, o + BU:o + BU + 8] = fm(b[OU:OU + 1024]); v[:, o + BXQ:o + BXQ + 8] = fm(b[OXQ:OXQ + 1024])
        v[:, o + BG:o + BG + 24] = fm(b[OG:OG + 3072])
        v[:, o + MHG:o + MHG + 8] = fm(inp["mh_g"][l]); v[:, o + PSC:o + PSC + 8] = fm(inp["pool_scale"][l])
        v[:, o + L1G:o + L1G + 8] = fm(inp["ln1_g"][l]); v[:, o + L1B:o + L1B + 8] = fm(inp["ln1_b"][l])
        v[:, o + L2G:o + L2G + 8] = fm(inp["ln2_g"][l]); v[:, o + L2B:o + L2B + 8] = fm(inp["ln2_b"][l])
        for j in range(3):
            v[:, o + CW + j * 44:o + CW + (j + 1) * 44] = fm(inp["conv_w"][l, j])
        v[:, o + CB:o + CB + 44] = fm(inp["conv_b"][l])
        v[0:4, o + BIG] = b[OIG:OIG + 4]; v[0:4, o + BFG] = b[OFG:OFG + 4]
    v[:, LIG:LIG + 8] = fm(inp["ln_in_g"]); v[:, LIB:LIB + 8] = fm(inp["ln_in_b"])
    return v


_NC_CACHE = {}


def kernel(**inp):
    depth = int(os.environ.get("KDEPTH", DEPTH))
    if depth not in _NC_CACHE:
        _NC_CACHE[depth] = build(depth)
    nc = _NC_CACHE[depth]
    inp = {k_: np.asarray(v) for k_, v in inp.items()}
    cst = _consts(); vecs = _vecs(inp)
    shared = dict(
        w_in=inp["w_in"], b_in=inp["b_in"], w_m_out=inp["w_m_out"], w_pool=inp["w_pool"].reshape(DEPTH, 1024, 256),
        w_kv=inp["w_mem_kv"], w_x_out=inp["w_x_out"], w_o=inp["w_o"], w_up=inp["w_up"], w_down=inp["w_down"], vecs=vecs, cst=cst)
    in_maps = []
    for b in range(8):
        js = slice(b * NS, (b + 1) * NS)
        m = dict(shared)
        m.update(
            xp=inp["x_prompt"][b], mem=inp["mem_prompt"][b], xs=inp["x_sample"][js, 0, :],
            cmk=inp["cache_mem_k"][:, js].reshape(DEPTH, NS, NMEM, D), cmv=inp["cache_mem_v"][:, js].reshape(DEPTH, NS, NMEM, D),
            sC=inp["state_C"][:, js], sn=inp["state_n"][:, js].reshape(DEPTH, NS, H * DH), sm=inp["state_m"][:, js],
            spool=inp["state_pool"][:, js], sconv=inp["state_conv"][:, js])
        in_maps.append({k_: np.ascontiguousarray(v, dtype=np.float32) for k_, v in m.items()})
    res = run_bass_kernel_spmd(nc, in_maps, core_ids=list(range(8)))
    R = res.results

    def cat(name, axis, shape=None):
        a = np.stack([np.asarray(r[name]) for r in R], axis=axis) if shape == "stack" else np.concatenate([np.asarray(r[name]) for r in R], axis=axis)
        return a
    y_p = np.stack([np.asarray(r["yp"]) for r in R], 0)
    y_s = np.concatenate([np.asarray(r["ys"]) for r in R], 0).reshape(128, 1, D)
    p_C = np.stack([np.asarray(r["pC"]) for r in R], 1)
    p_n = np.stack([np.asarray(r["pn"]).reshape(DEPTH, H, DH) for r in R], 1)
    p_m = np.stack([np.asarray(r["pm"]) for r in R], 1)
    p_pool = np.stack([np.asarray(r["ppool"]) for r in R], 1)
    p_conv = np.stack([np.asarray(r["pconv"]) for r in R], 1)
    p_mk = np.stack([np.asarray(r["pmk"]).reshape(DEPTH, NMEM, H, DH) for r in R], 1)
    p_mv = np.stack([np.asarray(r["pmv"]).reshape(DEPTH, NMEM, H, DH) for r in R], 1)
    s_C = np.concatenate([np.asarray(r["sCo"]) for r in R], 1)
    s_n = np.concatenate([np.asarray(r["sno"]).reshape(DEPTH, NS, H, DH) for r in R], 1)
    s_m = np.concatenate([np.asarray(r["smo"]) for r in R], 1)
    s_pool = np.concatenate([np.asarray(r["spoolo"]) for r in R], 1)
    s_conv = np.concatenate([np.asarray(r["sconvo"]) for r in R], 1)
    outs = (y_p, y_s, p_C, p_n, p_m, p_pool, p_conv, p_mk, p_mv, s_C, s_n, s_m, s_pool, s_conv)
    return tuple(np.ascontiguousarray(o, dtype=np.float32) for o in outs)
```
